# Optimizing a Trainium2 kernel written in Bass

```python
import jax, jax.numpy as jnp
from jax import lax
import numpy as np

D_MODEL = 4096
BATCH = 32
SEQ = 256
DEPTH = 2
DEC_BATCH = 8
DEC_SEQ = 2048
PAST_LEN = 512

GRID_W = 64
N_EVEN = (DEPTH + 1) // 2
N_ODD = DEPTH // 2
FOURIER_WIDTH = D_MODEL // 2
N_FOURIER_GROUPS = 4
FOURIER_GROUP = FOURIER_WIDTH // N_FOURIER_GROUPS
POOL_WIDTH = D_MODEL - FOURIER_WIDTH
POOL_WINDOWS = (2, 4, 8, 16)
POOL_GROUP = POOL_WIDTH // len(POOL_WINDOWS)
HEAD_DIM = 128
N_HEADS = D_MODEL // HEAD_DIM
NA_ROWS_MAX = 8
NA_COLS = 16
KEY_COLS = 2 * NA_COLS
N_COL_BLOCKS = GRID_W // NA_COLS
CTX_BLOCK = 128
D_FF = 11008
CONV_WIDTH = 3
ALPHA = float((2 * DEPTH) ** 0.25)
BETA = float((8 * DEPTH) ** -0.25)
LN_EPS = 1e-5
NEG = -1e30

kernel_name = "hybrid_fourier_pool_natten_diffusion_step"


def ln_plain(x):
    xf = x.astype(jnp.float32)
    mu = jnp.mean(xf, axis=-1, keepdims=True)
    var = jnp.mean(jnp.square(xf - mu), axis=-1, keepdims=True)
    return ((xf - mu) * lax.rsqrt(var + LN_EPS)).astype(x.dtype)


def ln_affine(x, g, b):
    xf = x.astype(jnp.float32)
    mu = jnp.mean(xf, axis=-1, keepdims=True)
    var = jnp.mean(jnp.square(xf - mu), axis=-1, keepdims=True)
    return ((xf - mu) * lax.rsqrt(var + LN_EPS) * g + b).astype(x.dtype)


def fourier_mix(u, w_four):
    B, L, _ = u.shape
    ug = u.reshape(B, L, N_FOURIER_GROUPS, FOURIER_GROUP).astype(jnp.float32)
    f = jnp.fft.fft2(ug, axes=(1, 3), norm="ortho").real.astype(u.dtype)
    y = jnp.einsum("blgc,gcd->blgd", f, w_four)
    return y.reshape(B, L, FOURIER_WIDTH)


def centred_mean_minus_self(u, w):
    L = u.shape[1]
    uf = u.astype(jnp.float32)
    cs = jnp.concatenate([jnp.zeros_like(uf[:, :1]), jnp.cumsum(uf, axis=1)], axis=1)
    t = np.arange(L)
    lo = np.clip(t - w // 2, 0, L)
    hi = np.clip(t - w // 2 + w, 0, L)
    cnt = jnp.asarray((hi - lo).astype(np.float32))[None, :, None]
    return ((cs[:, hi] - cs[:, lo]) / cnt - uf).astype(u.dtype)


def pool_mix(u, w_pool, pool_scale):
    B, L, _ = u.shape
    ug = u.reshape(B, L, len(POOL_WINDOWS), POOL_GROUP)
    pooled = jnp.stack([centred_mean_minus_self(ug[:, :, g], w) for g, w in enumerate(POOL_WINDOWS)], axis=2)
    y = jnp.einsum("blgc,gcd->blgd", pooled, w_pool).reshape(B, L, POOL_WIDTH)
    return y * pool_scale


def fourier_pool_mixer(h, w_in, w_four, w_pool, pool_scale, w_out):
    u = h @ w_in
    ya = fourier_mix(u[..., :FOURIER_WIDTH], w_four)
    yb = pool_mix(u[..., FOURIER_WIDTH:], w_pool, pool_scale)
    return jnp.concatenate([ya, yb], axis=-1) @ w_out


def qkv_heads(h, w_qkv):
    B, L, _ = h.shape
    qkv = (h @ w_qkv).reshape(B, L, 3, N_HEADS, HEAD_DIM)
    return qkv[:, :, 0], qkv[:, :, 1], qkv[:, :, 2]


def context_attention(q, k, v):
    B, H, Lc, Dh = q.shape
    nb = Lc // CTX_BLOCK
    qb = (q * (Dh ** -0.5)).reshape(B, H, nb, CTX_BLOCK, Dh)

    def block(i):
        qi = lax.dynamic_index_in_dim(qb, i, axis=2, keepdims=False)
        s = jnp.einsum("bhqd,bhkd->bhqk", qi, k).astype(jnp.float32)
        p = jax.nn.softmax(s, axis=-1).astype(v.dtype)
        return jnp.einsum("bhqk,bhkd->bhqd", p, v)

    o = lax.map(block, jnp.arange(nb))
    return jnp.transpose(o, (1, 0, 3, 2, 4)).reshape(B, Lc, H * Dh)


def neighbourhood_attention(q, k, v, k_ctx, v_ctx, rpb):
    B, L, H, Dh = q.shape
    rows = L // GRID_W
    kr = min(NA_ROWS_MAX, rows)
    qg = (q * (Dh ** -0.5)).reshape(B, rows, N_COL_BLOCKS, NA_COLS, H, Dh)
    kg = k.reshape(B, rows, GRID_W, H, Dh)
    vg = v.reshape(B, rows, GRID_W, H, Dh)
    qcol = np.arange(GRID_W).reshape(N_COL_BLOCKS, NA_COLS)
    col_start = np.clip(qcol - NA_COLS // 2, 0, GRID_W - NA_COLS)
    kcol0 = np.clip(np.arange(N_COL_BLOCKS) * NA_COLS - NA_COLS // 2, 0, GRID_W - KEY_COLS)
    kcol = kcol0[:, None] + np.arange(KEY_COLS)
    col_mask = jnp.asarray((kcol[:, None, :] >= col_start[:, :, None]) &
                           (kcol[:, None, :] < col_start[:, :, None] + NA_COLS))
    dc_idx = np.clip(kcol[:, None, :] - qcol[:, :, None] + NA_COLS - 1, 0, 2 * NA_COLS - 2)
    rpb_col = rpb[:, :, dc_idx]

    def row_step(r):
        rs = jnp.clip(r - kr // 2, 0, rows - kr)
        kb = lax.dynamic_slice_in_dim(kg, rs, kr, axis=1)[:, :, kcol]
        vb = lax.dynamic_slice_in_dim(vg, rs, kr, axis=1)[:, :, kcol]
        qr = lax.dynamic_index_in_dim(qg, r, axis=1, keepdims=False)
        s_loc = jnp.einsum("bjqhd,bkjchd->bhjqkc", qr, kb).astype(jnp.float32)
        dr_idx = rs + jnp.arange(kr) - r + NA_ROWS_MAX - 1
        bias = jnp.transpose(rpb_col[:, dr_idx], (0, 2, 3, 1, 4))
        s_loc = jnp.where(col_mask[None, None, :, :, None, :], s_loc + bias[None], NEG)
        s_ctx = jnp.einsum("bjqhd,bhnd->bhjqn", qr, k_ctx).astype(jnp.float32)
        s = jnp.concatenate([s_loc.reshape(B, H, N_COL_BLOCKS, NA_COLS, kr * KEY_COLS), s_ctx], axis=-1)
        p = jax.nn.softmax(s, axis=-1).astype(v.dtype)
        p_loc = p[..., :kr * KEY_COLS].reshape(B, H, N_COL_BLOCKS, NA_COLS, kr, KEY_COLS)
        p_ctx = p[..., kr * KEY_COLS:]
        return (jnp.einsum("bhjqkc,bkjchd->bjqhd", p_loc, vb) +
                jnp.einsum("bhjqn,bhnd->bjqhd", p_ctx, v_ctx))

    o = lax.map(row_step, jnp.arange(rows))
    return jnp.moveaxis(o, 0, 1).reshape(B, L, H * Dh)


def conv_ffn(h, w_up, conv_w, conv_b, w_down):
    u = h @ w_up
    L = u.shape[1]
    pad = CONV_WIDTH // 2
    up = jnp.pad(u, ((0, 0), (pad, CONV_WIDTH - 1 - pad), (0, 0)))
    u = sum(up[:, j:j + L] * conv_w[j] for j in range(CONV_WIDTH)) + conv_b
    a, g = jnp.split(u, 2, axis=-1)
    return (jax.nn.silu(g) * a) @ w_down


def trunk(x, cond, ctx_k, ctx_v, w_ada, b_ada, ln1_g, ln1_b, ln2_g, ln2_b,
          w_in, w_four, w_pool, pool_scale, w_out_a, w_qkv, rpb, w_out_c,
          w_up, conv_w, conv_b, w_down):
    is_context = ctx_k is None
    new_k, new_v = [], []
    for i in range(DEPTH):
        mod = (jax.nn.silu(cond) @ w_ada[i] + b_ada[i])[:, None, :]
        sh1, sc1, g1, sh2, sc2, g2 = jnp.split(mod, 6, axis=-1)
        h = ln_plain(x) * (1 + sc1) + sh1
        j = i // 2
        if i % 2 == 0:
            y = fourier_pool_mixer(h, w_in[j], w_four[j], w_pool[j], pool_scale[j], w_out_a[j])
        else:
            q, k, v = qkv_heads(h, w_qkv[j])
            if is_context:
                kt = jnp.transpose(k, (0, 2, 1, 3))
                vt = jnp.transpose(v, (0, 2, 1, 3))
                o = context_attention(jnp.transpose(q, (0, 2, 1, 3)), kt, vt)
                new_k.append(kt)
                new_v.append(vt)
            else:
                o = neighbourhood_attention(q, k, v, ctx_k[:, j], ctx_v[:, j], rpb[j])
            y = o @ w_out_c[j]
        x = ln_affine(ALPHA * x + g1 * y, ln1_g[i], ln1_b[i])
        h = ln_plain(x) * (1 + sc2) + sh2
        f = conv_ffn(h, w_up[i], conv_w[i], conv_b[i], w_down[i])
        x = ln_affine(ALPHA * x + g2 * f, ln2_g[i], ln2_b[i])
    return x, new_k, new_v


def setup_inputs(seed: int = 0) -> dict:
    key = jax.random.key(seed)
    ks = jax.random.split(key, 32)
    D = D_MODEL
    nrm = lambda k, shape, s: jax.random.normal(k, shape, jnp.float32) * s
    return {
        "x_prompt": nrm(ks[0], (BATCH, SEQ, D), 1.0),
        "x_sample": nrm(ks[1], (DEC_BATCH, DEC_SEQ, D), 1.0),
        "cache_k": nrm(ks[2], (DEC_BATCH, N_ODD, N_HEADS, PAST_LEN, HEAD_DIM), 1.0),
        "cache_v": nrm(ks[3], (DEC_BATCH, N_ODD, N_HEADS, PAST_LEN, HEAD_DIM), 1.0),
        "c": nrm(ks[4], (DEC_BATCH, D), 1.0),
        "c_ctx": nrm(ks[5], (D,), 1.0),
        "w_ada": nrm(ks[6], (DEPTH, D, 6 * D), 0.5 * D ** -0.5),
        "b_ada": nrm(ks[7], (DEPTH, 6 * D), 0.01),
        "ln1_g": 1.0 + nrm(ks[8], (DEPTH, D), 0.02),
        "ln1_b": nrm(ks[9], (DEPTH, D), 0.02),
        "ln2_g": 1.0 + nrm(ks[10], (DEPTH, D), 0.02),
        "ln2_b": nrm(ks[11], (DEPTH, D), 0.02),
        "w_in": nrm(ks[12], (N_EVEN, D, D), D ** -0.5),
        "w_four": nrm(ks[13], (N_EVEN, N_FOURIER_GROUPS, FOURIER_GROUP, FOURIER_GROUP), FOURIER_GROUP ** -0.5),
        "w_pool": nrm(ks[14], (N_EVEN, len(POOL_WINDOWS), POOL_GROUP, POOL_GROUP), POOL_GROUP ** -0.5),
        "pool_scale": 1.0 + nrm(ks[15], (N_EVEN, POOL_WIDTH), 0.1),
        "w_out_a": nrm(ks[16], (N_EVEN, D, D), BETA * D ** -0.5),
        "w_qkv": nrm(ks[17], (N_ODD, D, 3 * D), D ** -0.5),
        "rpb": nrm(ks[18], (N_ODD, N_HEADS, 2 * NA_ROWS_MAX - 1, 2 * NA_COLS - 1), 0.1),
        "w_out_c": nrm(ks[19], (N_ODD, D, D), BETA * D ** -0.5),
        "w_up": nrm(ks[20], (DEPTH, D, 2 * D_FF), D ** -0.5),
        "conv_w": nrm(ks[21], (DEPTH, CONV_WIDTH, 2 * D_FF), CONV_WIDTH ** -0.5),
        "conv_b": nrm(ks[22], (DEPTH, 2 * D_FF), 0.01),
        "w_down": nrm(ks[23], (DEPTH, D_FF, D), BETA * D_FF ** -0.5),
    }


def reference(x_prompt, x_sample, cache_k, cache_v, c, c_ctx, w_ada, b_ada, ln1_g, ln1_b, ln2_g, ln2_b,
              w_in, w_four, w_pool, pool_scale, w_out_a, w_qkv, rpb, w_out_c,
              w_up, conv_w, conv_b, w_down):
    weights = (w_ada, b_ada, ln1_g, ln1_b, ln2_g, ln2_b, w_in, w_four, w_pool, pool_scale, w_out_a,
               w_qkv, rpb, w_out_c, w_up, conv_w, conv_b, w_down)
    y_prompt, ks_list, vs_list = trunk(x_prompt, c_ctx[None, :], None, None, *weights)
    new_k = jnp.stack(ks_list, axis=1)
    new_v = jnp.stack(vs_list, axis=1)
    y_sample, _, _ = trunk(x_sample, c, cache_k, cache_v, *weights)
    return (y_prompt, y_sample, new_k, new_v)
```

```python
import numpy as np
from contextlib import ExitStack
import concourse.bass as bass
import concourse.mybir as mybir
from concourse.bass_utils import run_bass_kernel_spmd

F32 = mybir.dt.float32
BF16 = mybir.dt.bfloat16
AF = mybir.ActivationFunctionType
ALU = mybir.AluOpType


class Cfg:
    def __init__(s, D=4096, DFF=11008, LS=2048, LP=256, NP=4, PAST=512, depth=2):
        s.D, s.DFF, s.LS, s.LP, s.NP, s.PAST, s.depth = D, DFF, LS, LP, NP, PAST, depth
        s.NH = D // 128
        s.KC = D // 128
        s.FC = DFF // 128
        s.TOK = LS + NP * LP
        s.ROWS = LS // 64
        s.FG = D // 8
        s.ALPHA = float((2 * depth) ** 0.25)


class Tok:
    __slots__ = ("w", "r")

    def __init__(s):
        s.w = None
        s.r = {}


class Sch:
    def __init__(s, nc, stack):
        s.nc = nc
        s.E = {"pe": nc.tensor, "act": nc.scalar, "dve": nc.vector, "pool": nc.gpsimd, "sp": nc.sync}
        s.sem, s.cnt = {}, {}
        for e in s.E:
            s.sem[e] = stack.enter_context(nc.semaphore("s_" + e))
            s.cnt[e] = 0
        s.dq, s.dqi = {}, {}
        for q, n in (("sp", 12), ("pool", 6), ("act", 4)):
            s.dq[q] = []
            s.dqi[q] = 0
            for i in range(n):
                k = ("d", q, i)
                s.sem[k] = stack.enter_context(nc.semaphore("d_%s%d" % (q, i)))
                s.cnt[k] = 0
                s.dq[q].append(k)
        s.known = {e: {} for e in s.E}

    def _deps(s, reads, writes):
        deps = {}

        def add(k, n):
            if deps.get(k, 0) < n:
                deps[k] = n

        for t in reads:
            if t.w is not None:
                add(*t.w)
        for t in writes:
            if t.w is not None:
                add(*t.w)
            for k, n in t.r.items():
                add(k, n)
        return deps

    def _wait(s, eng, deps):
        kn = s.known[eng]
        for k, n in deps.items():
            if k == "pe" and eng == "pe":
                continue
            if kn.get(k, 0) >= n:
                continue
            s.E[eng].wait_ge(s.sem[k], n * 16 if isinstance(k, tuple) else n)
            kn[k] = n

    def _mark(s, me, reads, writes):
        k, n = me
        for t in reads:
            if t.r.get(k, 0) < n:
                t.r[k] = n
        for t in writes:
            t.w = me
            t.r = {}

    def op(s, eng, fn, reads=(), writes=()):
        s._wait(eng, s._deps(reads, writes))
        ins = fn()
        s.cnt[eng] += 1
        ins.then_inc(s.sem[eng], 1)
        s._mark((eng, s.cnt[eng]), reads, writes)

    def dma(s, q, out, in_, reads=(), writes=(), **kw):
        pool = s.dq[q]
        key = pool[s.dqi[q]]
        s.dqi[q] = (s.dqi[q] + 1) % len(pool)
        deps = s._deps(reads, writes)
        if s.cnt[key] > 0:
            deps[key] = max(deps.get(key, 0), s.cnt[key])
        s._wait(q, deps)
        ins = s.E[q].dma_start(out=out, in_=in_, **kw)
        s.cnt[key] += 1
        ins.then_inc(s.sem[key], 16)
        s._mark((key, s.cnt[key]), reads, writes)

    def barrier(s):
        for eng in s.E:
            s._wait(eng, {k: n for k, n in s.cnt.items() if n > 0})


class Ctx:
    pass


_uid = [0]


def _nm(p):
    _uid[0] += 1
    return "%s_%d" % (p, _uid[0])


def sb(K, st, shape, dt, name="t"):
    return st.enter_context(K.nc.sbuf_tensor(_nm(name), list(shape), dt))


def psum(K, st, shape, dt, name="ps"):
    return st.enter_context(K.nc.psum_tensor(_nm(name), list(shape), dt))


def load_cols(K, st, dst, dst_tok, vec, n, pst, pst_tok):
    S, nc = K.S, K.nc
    tmp = sb(K, st, [128, 128], F32, "lc")
    ttok = Tok()
    v2 = vec.rearrange("(n p) -> n p", p=128)
    j0 = 0
    while j0 < n:
        m = min(128, n - j0)
        S.dma("sp", tmp[0:m, :], v2[j0:j0 + m, :], writes=[ttok])
        S.op("pe", lambda: nc.tensor.transpose(out=pst[:, 0:m], in_=tmp[0:m, :], identity=K.identF[0:m, 0:m]),
             reads=[ttok, K.const_tok], writes=[pst_tok])
        S.op("dve", lambda: nc.vector.tensor_copy(out=dst[:, j0:j0 + m], in_=pst[:, 0:m]),
             reads=[pst_tok], writes=[dst_tok])
        j0 += m


def load_aT(K, aT, atoks, src, KCn, t0, T):
    v = src.rearrange("(kc p) t -> p kc t", p=128)
    ng = len(atoks)
    per = (KCn + ng - 1) // ng
    for g in range(ng):
        a, b = g * per, min(KCn, (g + 1) * per)
        if a >= b:
            continue
        K.S.dma("sp", aT[:, a:b, 0:T], v[:, a:b, t0:t0 + T], writes=[atoks[g]])


class WStream:
    def __init__(s, K, st, KCmax, WNmax, nbuf=2):
        s.K = K
        s.buf = [sb(K, st, [128, KCmax, WNmax], BF16, "w") for _ in range(nbuf)]
        s.tok = [Tok() for _ in range(nbuf)]
        s.i = 0

    def load(s, src, KCn, wn):
        b = s.i % len(s.buf)
        s.i += 1
        s.K.S.dma("pool", s.buf[b][:, 0:KCn, 0:wn], src.rearrange("(kc p) n -> p kc n", p=128), writes=[s.tok[b]])
        return s.buf[b], s.tok[b]


def col_tiles(n0, n1, wn):
    out = []
    a = n0
    while a < n1:
        out.append((a, min(wn, n1 - a)))
        a += wn
    return out


def gemm_fm(K, aT, atoks, KCn, T, wsrcs, ws, ps, pstoks, epi):
    S, nc = K.S, K.nc
    nb = (T + 511) // 512
    nslots = len(pstoks) // nb
    g = 0
    nxt = ws.load(wsrcs[0][0], KCn, wsrcs[0][1])
    for i, (src, wn, tag) in enumerate(wsrcs):
        wb, wt = nxt
        if i + 1 < len(wsrcs):
            nxt = ws.load(wsrcs[i + 1][0], KCn, wsrcs[i + 1][1])
        for m in range(wn // 128):
            slot = g % nslots
            g += 1
            banks = list(range(slot * nb, slot * nb + nb))

            def mm():
                ins = None
                for tb in range(nb):
                    tw = min(512, T - tb * 512)
                    for kc in range(KCn):
                        ins = nc.tensor.matmul(ps[:, banks[tb], 0:tw], wb[:, kc, m * 128:(m + 1) * 128],
                                               aT[:, kc, tb * 512:tb * 512 + tw], start=(kc == 0), stop=(kc == KCn - 1))
                return ins

            toks = [pstoks[b] for b in banks]
            S.op("pe", mm, reads=list(atoks) + (wt if isinstance(wt, list) else [wt]), writes=toks)
            pv = ps[:, banks[0]:banks[0] + nb, :].rearrange("p b n -> p (b n)")[:, 0:T]
            epi(tag, m, pv, toks)


def gemm_tm(K, aT, atoks, KCn, T, wsrcs, ws, ps, pstoks, epi):
    S, nc = K.S, K.nc
    g = 0
    nxt = ws.load(wsrcs[0][0], KCn, wsrcs[0][1])
    for i, (src, wn, tag) in enumerate(wsrcs):
        wb, wt = nxt
        if i + 1 < len(wsrcs):
            nxt = ws.load(wsrcs[i + 1][0], KCn, wsrcs[i + 1][1])
        for tb in range((T + 127) // 128):
            M = min(128, T - tb * 128)
            bank = g % len(pstoks)
            g += 1

            def mm():
                ins = None
                for kc in range(KCn):
                    ins = nc.tensor.matmul(ps[0:M, bank, 0:wn], aT[:, kc, tb * 128:tb * 128 + M], wb[:, kc, 0:wn],
                                           start=(kc == 0), stop=(kc == KCn - 1))
                return ins

            S.op("pe", mm, reads=list(atoks) + (wt if isinstance(wt, list) else [wt]), writes=[pstoks[bank]])
            epi(tag, tb, M, ps[0:M, bank, 0:wn], pstoks[bank])


class Evac:
    def __init__(s, K):
        s.K = K
        s.i = 0

    def copy(s, out, in_, reads, writes, scale=None):
        K = s.K
        s.i += 1
        if s.i % 2 == 0 or scale is not None:
            if scale is None:
                K.S.op("act", lambda: K.nc.scalar.copy(out=out, in_=in_), reads=reads, writes=writes)
            else:
                K.S.op("act", lambda: K.nc.scalar.activation(out=out, in_=in_, func=AF.Copy, scale=scale), reads=reads, writes=writes)
        else:
            K.S.op("dve", lambda: K.nc.vector.tensor_copy(out=out, in_=in_), reads=reads, writes=writes)


def phase_consts(K, st):
    S, nc = K.S, K.nc
    K.identF = sb(K, st, [128, 128], F32, "identF")
    K.identB = sb(K, st, [128, 128], BF16, "identB")
    K.onesB = sb(K, st, [128, 128], BF16, "onesB")
    K.const_tok = Tok()
    S.dma("sp", K.identF[:], K.d["ident"], writes=[K.const_tok])
    S.op("dve", lambda: nc.vector.tensor_copy(out=K.identB[:], in_=K.identF[:]), reads=[K.const_tok], writes=[K.const_tok])
    S.op("dve", lambda: nc.vector.memset(K.onesB[:], 1.0), writes=[K.const_tok])
    S.barrier()


def phase_mod(K):
    S, nc, C = K.S, K.nc, K.cfg
    with ExitStack() as st:
        pst = psum(K, st, [128, 128], F32)
        pst_tok = Tok()
        ps = psum(K, st, [128, 4, 512], F32)
        pstoks = [Tok() for _ in range(4)]
        cc = sb(K, st, [128, C.KC, 2], F32, "cc")
        cct = Tok()
        aT = sb(K, st, [128, C.KC, 2], BF16, "aTm")
        at = Tok()
        for c in range(2):
            tmpc = sb(K, st, [128, C.KC], F32, "tmpc")
            tt = Tok()
            load_cols(K, st, tmpc, tt, K.d["cond"][c, :], C.KC, pst, pst_tok)
            S.op("act", lambda: nc.scalar.activation(out=cc[:, :, c], in_=tmpc[:], func=AF.Silu), reads=[tt], writes=[cct])
        S.op("dve", lambda: nc.vector.tensor_copy(out=aT[:], in_=cc[:]), reads=[cct], writes=[at])
        ws = WStream(K, st, C.KC, 512)
        bt = [sb(K, st, [2, 512], F32, "bt") for _ in range(2)]
        btok = [Tok(), Tok()]
        ot = [sb(K, st, [2, 512], F32, "ot") for _ in range(2)]
        otok = [Tok(), Tok()]
        cnt = [0]
        for l in range(C.depth):
            wsrcs = [(K.d["w_ada"][l, :, n0:n0 + wn], wn, n0) for n0, wn in col_tiles(0, 6 * C.D, 512)]

            def epi(n0, tb, M, pv, ptok, l=l):
                i = cnt[0] % 2
                cnt[0] += 1
                S.dma("sp", bt[i][:, :], K.d["b_ada"][l, n0:n0 + 512].partition_broadcast(2), writes=[btok[i]])
                S.op("dve", lambda: nc.vector.tensor_tensor(out=ot[i][:, :], in0=pv, in1=bt[i][:, :], op=ALU.add),
                     reads=[ptok, btok[i]], writes=[otok[i]])
                S.dma("sp", K.modD[l, :, n0:n0 + 512], ot[i][:, :], reads=[otok[i]])

            gemm_tm(K, aT, [at], C.KC, 2, wsrcs, ws, ps, pstoks, epi)
    S.barrier()


def ln_pass(K, tiles, post, pre, xsrc, xdst):
    S, nc, C = K.S, K.nc, K.cfg
    D, KC = C.D, C.KC
    NST = (D + 511) // 512
    with ExitStack() as st:
        pT = [psum(K, st, [128, KC, 128], BF16) for _ in range(2)] if pre else None
        pTtok = [Tok(), Tok()]
        bc = {}
        names = (["gam", "bet"] if post else []) + (["sc", "sh"] if pre else [])
        for n in names:
            bc[n] = (sb(K, st, [128, D], F32, "bc" + n), Tok())
        NX = 3
        xt = [(sb(K, st, [128, D], F32, "xt"), Tok()) for _ in range(NX)]
        yt = [(sb(K, st, [128, D], F32, "yt"), Tok()) for _ in range(2)] if post else None
        hn = (sb(K, st, [128, D], F32, "hn"), Tok()) if pre else None
        hb = [(sb(K, st, [128, D], BF16, "hb"), Tok()) for _ in range(2)] if pre else None
        hTb = [(sb(K, st, [128, KC, 128], BF16, "hTb"), Tok()) for _ in range(2)] if pre else None
        stt = [sb(K, st, [128, NST, 6], F32, "stt") for _ in range(2)]
        stoks = [[Tok() for _ in range(NST)] for _ in range(2)]
        mv = [sb(K, st, [128, 4], F32, "mv") for _ in range(2)]
        mvt = [Tok(), Tok()]
        if post:
            l, gi, gam, bet = post
            S.dma("sp", bc["gam"][0][:], gam.partition_broadcast(128), writes=[bc["gam"][1]])
            S.dma("sp", bc["bet"][0][:], bet.partition_broadcast(128), writes=[bc["bet"][1]])

        def load_mod(c):
            l2, sci, shi = pre
            S.dma("sp", bc["sc"][0][:], K.modD[l2, c, sci * D:(sci + 1) * D].partition_broadcast(128), writes=[bc["sc"][1]])
            S.dma("sp", bc["sh"][0][:], K.modD[l2, c, shi * D:(shi + 1) * D].partition_broadcast(128), writes=[bc["sh"][1]])
            S.op("dve", lambda: nc.vector.tensor_scalar_add(out=bc["sc"][0][:], in0=bc["sc"][0][:], scalar1=1.0),
                 reads=[], writes=[bc["sc"][1]])

        def stats(x, xtok, w):
            sT, sK, m, mt = stt[w], stoks[w], mv[w], mvt[w]
            for j in range(NST):
                wd = min(512, D - j * 512)
                S.op("dve", lambda: nc.vector.bn_stats(out=sT[:, j, :], in_=x[:, j * 512:j * 512 + wd]),
                     reads=[xtok], writes=[sK[j]])
            S.op("dve", lambda: nc.vector.bn_aggr(out=m[:, 0:2], in_=sT[:].rearrange("p a b -> p (a b)")),
                 reads=sK, writes=[mt])
            S.op("dve", lambda: nc.vector.tensor_scalar_add(out=m[:, 2:3], in0=m[:, 1:2], scalar1=1e-5), reads=[mt], writes=[mt])
            S.op("act", lambda: nc.scalar.sqrt(out=m[:, 2:3], in_=m[:, 2:3]), reads=[mt], writes=[mt])
            S.op("dve", lambda: nc.vector.reciprocal(out=m[:, 2:3], in_=m[:, 2:3]), reads=[mt], writes=[mt])
            S.op("dve", lambda: nc.vector.scalar_tensor_tensor(out=m[:, 3:4], in0=m[:, 0:1], scalar=-1.0, in1=m[:, 2:3],
                                                               op0=ALU.mult, op1=ALU.mult), reads=[mt], writes=[mt])

        blocks = []
        for tl in tiles:
            for b_ in range(tl["T"] // 128):
                blocks.append((tl["t0"] + b_ * 128, tl["cond"]))

        def A1(i):
            t0, c = blocks[i]
            x, xtok = xt[i % NX]
            S.dma("sp", x[:], xsrc(t0, 128), writes=[xtok])
            if post:
                y, ytok = yt[i % 2]
                S.dma("sp", y[:], K.ypre[t0:t0 + 128, :], writes=[ytok])
                S.op("dve", lambda: nc.vector.scalar_tensor_tensor(out=x[:], in0=x[:], scalar=C.ALPHA, in1=y[:],
                                                                   op0=ALU.mult, op1=ALU.add), reads=[ytok], writes=[xtok])
                stats(x, xtok, 0)
                S.op("act", lambda: nc.scalar.activation(out=x[:], in_=x[:], func=AF.Identity, scale=mv[0][:, 2:3], bias=mv[0][:, 3:4]),
                     reads=[mvt[0]], writes=[xtok])

        def A2(i):
            t0, c = blocks[i]
            x, xtok = xt[i % NX]
            if post:
                S.op("dve", lambda: nc.vector.tensor_tensor(out=x[:], in0=x[:], in1=bc["gam"][0][:], op=ALU.mult),
                     reads=[bc["gam"][1]], writes=[xtok])
                S.op("dve", lambda: nc.vector.tensor_tensor(out=x[:], in0=x[:], in1=bc["bet"][0][:], op=ALU.add),
                     reads=[bc["bet"][1]], writes=[xtok])
                S.dma("sp", xdst(t0, 128), x[:], reads=[xtok])

        def B1(i):
            x, xtok = xt[i % NX]
            stats(x, xtok, 1)
            S.op("act", lambda: nc.scalar.activation(out=hn[0][:], in_=x[:], func=AF.Identity, scale=mv[1][:, 2:3], bias=mv[1][:, 3:4]),
                 reads=[mvt[1], xtok], writes=[hn[1]])

        def B2(i):
            t0, c = blocks[i]
            h_b, hbt = hb[i % 2]
            p_T, ptk = pT[i % 2], pTtok[i % 2]
            hT_b, hTt = hTb[i % 2]
            S.op("dve", lambda: nc.vector.tensor_tensor(out=hn[0][:], in0=hn[0][:], in1=bc["sc"][0][:], op=ALU.mult),
                 reads=[bc["sc"][1]], writes=[hn[1]])
            S.op("pool", lambda: nc.gpsimd.tensor_tensor(out=h_b[:], in0=hn[0][:], in1=bc["sh"][0][:], op=ALU.add),
                 reads=[bc["sh"][1], hn[1]], writes=[hbt])

            def tr():
                ins = None
                for kc in range(KC):
                    ins = nc.tensor.transpose(out=p_T[:, kc, :], in_=h_b[:, kc * 128:(kc + 1) * 128], identity=K.identB[:])
                return ins

            S.op("pe", tr, reads=[hbt, K.const_tok], writes=[ptk])
            S.op("act", lambda: nc.scalar.copy(out=hT_b[:], in_=p_T[:]), reads=[ptk], writes=[hTt])
            S.dma("sp", K.hT.rearrange("(kc p) t -> p kc t", p=128)[:, :, t0:t0 + 128], hT_b[:], reads=[hTt])

        nblk = len(blocks)
        cur = None
        for i in range(nblk + 1):
            if i < nblk:
                A1(i)
            if pre and i >= 1:
                if blocks[i - 1][1] != cur:
                    cur = blocks[i - 1][1]
                    load_mod(cur)
                B1(i - 1)
            if i < nblk:
                A2(i)
            if pre and i >= 1:
                B2(i - 1)
    S.barrier()


def phase_ffn(K, layer):
    S, nc, C = K.S, K.nc, K.cfg
    D, KC, DFF, FC = C.D, C.KC, C.DFF, C.FC
    with ExitStack() as st:
        cws = [sb(K, st, [128, 2 * FC], F32, "cw") for _ in range(4)]
        cwt = [Tok() for _ in range(4)]
        with ExitStack() as st2:
            pst = psum(K, st2, [128, 128], F32)
            pst_tok = Tok()
            for j in range(4):
                t, tk = cws[j], cwt[j]
                src = K.d["conv_w"][layer, j, :] if j < 3 else K.d["conv_b"][layer, :]
                load_cols(K, st2, t, tk, src, 2 * FC, pst, pst_tok)
            S.barrier()
        Tmax = max(tl["T"] for tl in K.tiles)
        ps = psum(K, st, [128, 8, 512], F32)
        pstoks = [Tok() for _ in range(8)]
        aT = sb(K, st, [128, KC, Tmax], BF16, "aT")
        atoks = [Tok() for _ in range(4)]
        ws = WStream(K, st, KC, 256)
        HW = min(1024, Tmax)
        ta = sb(K, st, [128, Tmax], F32, "ta")
        tg = sb(K, st, [128, HW], F32, "tg")
        tat, tgt = Tok(), Tok()
        ob = [sb(K, st, [128, HW], BF16, "ob") for _ in range(2)]
        obt = [Tok(), Tok()]
        oi = [0]
        wup = K.d["w_up"][layer].rearrange("k (two f) -> k two f", two=2)
        wtok2 = [[Tok(), Tok()], [Tok(), Tok()]]
        for tl in K.tiles:
            T, L, nseq, t0 = tl["T"], tl["L"], tl["nseq"], tl["t0"]
            load_aT(K, aT, atoks, K.hT, KC, t0, T)
            wsrcs = []
            for f0 in range(0, DFF, 128):
                wsrcs.append((wup[:, :, f0:f0 + 128], 256, f0))
            state = {}

            def conv(dst, dtok, pv, ptoks, col, h0, hw):
                w0, w1, w2, bb = (cws[j][:, col:col + 1] for j in range(4))
                S.op("act", lambda: nc.scalar.activation(out=dst[:, 0:hw], in_=pv[:, h0:h0 + hw], func=AF.Identity, scale=w1, bias=bb),
                     reads=ptoks + [cwt[1], cwt[3]], writes=[dtok])
                s = (h0 // L) * L
                while s < h0 + hw:
                    a, e = max(s, h0), min(s + L, h0 + hw)
                    a1 = max(a, s + 1)
                    if e > a1:
                        S.op("dve", lambda: nc.vector.scalar_tensor_tensor(out=dst[:, a1 - h0:e - h0], in0=pv[:, a1 - 1:e - 1], scalar=w0,
                                                                           in1=dst[:, a1 - h0:e - h0], op0=ALU.mult, op1=ALU.add),
                             reads=ptoks + [cwt[0]], writes=[dtok])
                    e2 = min(e, s + L - 1)
                    if e2 > a:
                        S.op("dve", lambda: nc.vector.scalar_tensor_tensor(out=dst[:, a - h0:e2 - h0], in0=pv[:, a + 1:e2 + 1], scalar=w2,
                                                                           in1=dst[:, a - h0:e2 - h0], op0=ALU.mult, op1=ALU.add),
                             reads=ptoks + [cwt[2]], writes=[dtok])
                    s += L

            def epi(f0, m, pv, ptoks, T=T, t0=t0):
                fc = f0 // 128
                if m == 0:
                    for h0 in range(0, T, HW):
                        hw = min(HW, T - h0)
                        conv(ta[:, h0:h0 + hw], tat, pv, ptoks, fc, h0, hw)
                    return
                for h0 in range(0, T, HW):
                    hw = min(HW, T - h0)
                    conv(tg, tgt, pv, ptoks, FC + fc, h0, hw)
                    S.op("act", lambda: nc.scalar.activation(out=tg[:, 0:hw], in_=tg[:, 0:hw], func=AF.Silu), reads=[], writes=[tgt])
                    i = oi[0] % 2
                    oi[0] += 1
                    S.op("dve", lambda: nc.vector.tensor_tensor(out=ob[i][:, 0:hw], in0=ta[:, h0:h0 + hw], in1=tg[:, 0:hw], op=ALU.mult),
                         reads=[tat, tgt], writes=[obt[i]])
                    S.dma("sp", K.actT[f0:f0 + 128, t0 + h0:t0 + h0 + hw], ob[i][:, 0:hw], reads=[obt[i]])

            class WS2:
                def load(s2, src, KCn, wn):
                    b = ws.i % len(ws.buf)
                    ws.i += 1
                    for two in range(2):
                        S.dma("pool", ws.buf[b][:, 0:KCn, two * 128:(two + 1) * 128],
                              src[:, two, :].rearrange("(kc p) f -> p kc f", p=128), writes=[wtok2[b][two]])
                    return ws.buf[b], wtok2[b]

            gemm_fm(K, aT, atoks, KC, T, wsrcs, WS2(), ps, pstoks, epi)
    S.barrier()
    with ExitStack() as st:
        TD = 512
        ps = psum(K, st, [128, 8, 512], F32)
        pstoks = [Tok() for _ in range(8)]
        aT = sb(K, st, [128, FC, TD], BF16, "aTd")
        atoks = [Tok() for _ in range(4)]
        KH = [(0, FC // 2), (FC // 2, FC)]
        KHmax = max(b_ - a_ for a_, b_ in KH)
        ws = WStream(K, st, KHmax, 512)
        ob = [sb(K, st, [128, 512], F32, "obd") for _ in range(2)]
        obt = [Tok() for _ in range(2)]
        oi = [0]
        gb = sb(K, st, [128, D], F32, "gbd")
        gbt = Tok()
        cur = [None]
        wd = K.d["w_down"][layer]
        units = []
        for tt0 in range(0, C.TOK, TD):
            for n0, wn in col_tiles(0, D, 512):
                for hi, (ka, kb) in enumerate(KH):
                    units.append((tt0, n0, wn, hi, ka, kb))
        nxt = ws.load(wd[units[0][4] * 128:units[0][5] * 128, units[0][1]:units[0][1] + units[0][2]], units[0][5] - units[0][4], units[0][2])
        slot = 0
        for ui, (tt0, n0, wn, hi, ka, kb) in enumerate(units):
            T = min(TD, C.TOK - tt0)
            NTB = (T + 127) // 128
            if n0 == 0 and hi == 0:
                cnd = 0 if tt0 < C.LS else 1
                if cur[0] != cnd:
                    cur[0] = cnd
                    S.dma("sp", gb[:], K.modD[layer, cnd, 5 * D:6 * D].partition_broadcast(128), writes=[gbt])
                load_aT(K, aT, atoks, K.actT, FC, tt0, T)
            wb, wt = nxt
            if ui + 1 < len(units):
                u2 = units[ui + 1]
                nxt = ws.load(wd[u2[4] * 128:u2[5] * 128, u2[1]:u2[1] + u2[2]], u2[5] - u2[4], u2[2])
            if hi == 0:
                slot = (slot + 1) % 2
            for tb in range(NTB):
                M = min(128, T - tb * 128)
                bank = slot * 4 + tb

                def mm():
                    ins = None
                    for kc in range(ka, kb):
                        ins = nc.tensor.matmul(ps[0:M, bank, 0:wn], aT[:, kc, tb * 128:tb * 128 + M], wb[:, kc - ka, 0:wn],
                                               start=(kc == 0), stop=(kc == FC - 1))
                    return ins

                S.op("pe", mm, reads=list(atoks) + [wt], writes=[pstoks[bank]])
                if hi == len(KH) - 1:
                    i = oi[0] % 2
                    oi[0] += 1
                    S.op("dve", lambda: nc.vector.tensor_tensor(out=ob[i][0:M, 0:wn], in0=ps[0:M, bank, 0:wn], in1=gb[0:M, n0:n0 + wn], op=ALU.mult),
                         reads=[pstoks[bank], gbt], writes=[obt[i]])
                    S.dma("sp", K.ypre[tt0 + tb * 128:tt0 + tb * 128 + M, n0:n0 + wn], ob[i][0:M, 0:wn], reads=[obt[i]])
    S.barrier()


def gemm_out_tm(K, srcT, w_ap, layer, gi):
    S, nc, C = K.S, K.nc, K.cfg
    D, KC = C.D, C.KC
    with ExitStack() as st:
        Tmax = max(tl["T"] for tl in K.tiles)
        ps = psum(K, st, [128, 8, 512], F32)
        pstoks = [Tok() for _ in range(8)]
        aT = sb(K, st, [128, KC, Tmax], BF16, "aTo")
        atoks = [Tok() for _ in range(4)]
        ws = WStream(K, st, KC, 256)
        ob = [sb(K, st, [128, 512], F32, "obo") for _ in range(4)]
        obt = [Tok() for _ in range(4)]
        oi = [0]
        gb = sb(K, st, [128, D], F32, "gbo")
        gbt = Tok()
        for tl in K.tiles:
            T, t0 = tl["T"], tl["t0"]
            S.dma("sp", gb[:], K.modD[layer, tl["cond"], gi * D:(gi + 1) * D].partition_broadcast(128), writes=[gbt])
            load_aT(K, aT, atoks, srcT, KC, t0, T)
            wsrcs = [(w_ap[:, n0:n0 + wn], wn, n0) for n0, wn in col_tiles(0, D, 256)]

            def epi(n0, tb, M, pv, ptok, t0=t0):
                i = oi[0] % 4
                oi[0] += 1
                wn = pv.shape[-1]
                S.op("dve", lambda: nc.vector.tensor_tensor(out=ob[i][0:M, 0:wn], in0=pv, in1=gb[0:M, n0:n0 + wn], op=ALU.mult),
                     reads=[ptok, gbt], writes=[obt[i]])
                S.dma("sp", K.ypre[t0 + tb * 128:t0 + tb * 128 + M, n0:n0 + wn], ob[i][0:M, 0:wn], reads=[obt[i]])

            gemm_tm(K, aT, atoks, KC, T, wsrcs, ws, ps, pstoks, epi)
    S.barrier()


def phase_fourier_weights(K):
    S, nc, C = K.S, K.nc, K.cfg
    FG = C.FG
    GC = FG // 128
    with ExitStack() as st:
        ps = psum(K, st, [128, 8, 512], F32)
        pstoks = [Tok() for _ in range(8)]
        cs = sb(K, st, [128, 2, GC, FG], BF16, "cs")
        cst = [Tok(), Tok()]
        for j, nm in enumerate(("CC", "SC")):
            S.dma("pool", cs[:, j, :, :], K.d[nm].rearrange("(kc p) n -> p kc n", p=128), writes=[cst[j]])
        ws = WStream(K, st, GC, 512)
        ob = [sb(K, st, [128, 512], F32, "obw") for _ in range(4)]
        obt = [Tok() for _ in range(4)]
        oi = [0]
        ev = Evac(K)
        for g in range(4):
            for j in range(2):
                wsrcs = [(K.d["w_four"][g, :, n0:n0 + wn], wn, n0) for n0, wn in col_tiles(0, FG, 512)]

                def epi(n0, tb, M, pv, ptok, g=g, j=j):
                    i = oi[0] % 4
                    oi[0] += 1
                    wn = pv.shape[-1]
                    ev.copy(ob[i][0:M, 0:wn], pv, [ptok], [obt[i]])
                    S.dma("sp", K.WcD[j, g, tb * 128:tb * 128 + M, n0:n0 + wn], ob[i][0:M, 0:wn], reads=[obt[i]])

                gemm_tm(K, cs[:, j, :, :], [cst[j]], GC, FG, wsrcs, ws, ps, pstoks, epi)
    S.barrier()


def phase_win(K):
    S, nc, C = K.S, K.nc, K.cfg
    D, KC, FG = C.D, C.KC, C.FG
    with ExitStack() as st:
        Tmax = max(tl["T"] for tl in K.tiles)
        ps = psum(K, st, [128, 8, 512], F32)
        pstoks = [Tok() for _ in range(8)]
        aT = sb(K, st, [128, KC, Tmax], BF16, "aTw")
        atoks = [Tok() for _ in range(4)]
        ws = WStream(K, st, KC, 256)
        ob = [sb(K, st, [128, Tmax], BF16, "obw") for _ in range(2)]
        obt = [Tok(), Tok()]
        oi = [0]
        ev = Evac(K)
        PADW = max(tl["nseq"] * (tl["L"] + 16) for tl in K.tiles)
        U = sb(K, st, [128, PADW], F32, "U")
        P1 = sb(K, st, [128, PADW], F32, "P1")
        P2 = sb(K, st, [128, PADW], F32, "P2")
        Ut, P1t, P2t = Tok(), Tok(), Tok()
        E1 = sb(K, st, [128, 4 * 8], F32, "E1")
        E1t = Tok()
        edge = sb(K, st, [128, 4, 2, 8], F32, "edge")
        edget = Tok()
        S.dma("sp", edge[:].rearrange("p a b c -> p (a b c)"), K.d["pool_edge"].partition_broadcast(128), writes=[edget])
        for tl in K.tiles:
            T, L, nseq, t0 = tl["T"], tl["L"], tl["nseq"], tl["t0"]
            Lp = L + 16
            load_aT(K, aT, atoks, K.hT, KC, t0, T)
            S.op("dve", lambda: nc.vector.memset(U[:], 0.0), writes=[Ut])
            wsrcs = [(K.d["w_in"][:, n0:n0 + wn], wn, n0) for n0, wn in col_tiles(0, D, 256)]
            U3 = U[:, 0:nseq * Lp].rearrange("p (s l) -> p s l", s=nseq)
            P13 = P1[:, 0:nseq * Lp].rearrange("p (s l) -> p s l", s=nseq)
            P23 = P2[:, 0:nseq * Lp].rearrange("p (s l) -> p s l", s=nseq)

            def epi(n0, m, pv, ptoks, T=T, L=L, nseq=nseq, t0=t0, Lp=Lp, U3=U3, P13=P13, P23=P23):
                f0 = n0 + m * 128
                i = oi[0] % 2
                oi[0] += 1
                o3 = ob[i][:, 0:T].rearrange("p (s l) -> p s l", s=nseq)
                if f0 < D // 2:
                    ev.copy(ob[i][:, 0:T], pv, ptoks, [obt[i]])
                else:
                    lv = (f0 - D // 2) // FG + 1
                    w = 1 << lv
                    pv3 = pv.rearrange("p (s l) -> p s l", s=nseq)
                    S.op("act", lambda: nc.scalar.copy(out=U3[:, :, 8:8 + L], in_=pv3), reads=ptoks, writes=[Ut])
                    src, srct = U3, Ut
                    bufs = [(P13, P1t), (P23, P2t)]
                    for k in range(1, lv + 1):
                        dst, dstt = bufs[k % 2]
                        sh = 1 << (k - 1)
                        n = Lp - (1 << k) + 1
                        S.op("dve", lambda: nc.vector.tensor_tensor(out=dst[:, :, 0:n], in0=src[:, :, 0:n], in1=src[:, :, sh:sh + n], op=ALU.add),
                             reads=[srct], writes=[dstt])
                        src, srct = dst, dstt
                    o = 8 - w // 2
                    S.op("dve", lambda: nc.vector.scalar_tensor_tensor(out=o3, in0=src[:, :, o:o + L], scalar=1.0 / w, in1=U3[:, :, 8:8 + L],
                                                                       op0=ALU.mult, op1=ALU.subtract), reads=[srct, Ut], writes=[obt[i]])
                    hl = w // 2
                    for s_ in range(nseq):
                        S.op("dve", lambda: nc.vector.tensor_tensor(out=E1[:, 0:hl], in0=src[:, s_, o:o + hl], in1=edge[:, lv - 1, 0, 0:hl], op=ALU.mult),
                             reads=[srct, edget], writes=[E1t])
                        S.op("dve", lambda: nc.vector.tensor_tensor(out=o3[:, s_, 0:hl], in0=E1[:, 0:hl], in1=U3[:, s_, 8:8 + hl], op=ALU.subtract),
                             reads=[E1t, Ut], writes=[obt[i]])
                        if hl > 1:
                            a = L - hl + 1
                            S.op("dve", lambda: nc.vector.tensor_tensor(out=E1[:, 8:8 + hl - 1], in0=src[:, s_, o + a:o + L],
                                                                        in1=edge[:, lv - 1, 1, 8 - (hl - 1):8], op=ALU.mult),
                                 reads=[srct, edget], writes=[E1t])
                            S.op("dve", lambda: nc.vector.tensor_tensor(out=o3[:, s_, a:L], in0=E1[:, 8:8 + hl - 1], in1=U3[:, s_, 8 + a:8 + L], op=ALU.subtract),
                                 reads=[E1t, Ut], writes=[obt[i]])
                S.dma("sp", K.uT[f0:f0 + 128, t0:t0 + T], ob[i][:, 0:T], reads=[obt[i]])

            gemm_fm(K, aT, atoks, KC, T, wsrcs, ws, ps, pstoks, epi)
    S.barrier()


def phase_fourier(K):
    S, nc, C = K.S, K.nc, K.cfg
    FG = C.FG
    GC = FG // 128
    for tl in K.tiles:
        T, L, nseq, t0 = tl["T"], tl["L"], tl["nseq"], tl["t0"]
        LC = L // 128
        nb = (L + 511) // 512
        with ExitStack() as st:
            ps = psum(K, st, [128, 8, 512], F32)
            pstoks = [Tok() for _ in range(8)]
            cl = sb(K, st, [128, 2, LC, L], BF16, "cl")
            clt = [Tok(), Tok()]
            for j, nm in enumerate(("CL", "SLn")):
                srcm = K.d[nm + tl["name"]].rearrange("(kc p) n -> p kc n", p=128)
                for n0, wn in col_tiles(0, L, 512):
                    S.dma("pool", cl[:, j, :, n0:n0 + wn], srcm[:, :, n0:n0 + wn], writes=[clt[j]])
            uT = [sb(K, st, [128, GC, L], BF16, "uTg") for _ in range(1)]
            uTt = [Tok()]
            wcs = [sb(K, st, [128, 2, GC, FG], BF16, "wcs") for _ in range(1)]
            wcst = [Tok()]
            Pcs = sb(K, st, [128, 2, LC, FG], BF16, "Pcs")
            Pt = [Tok(), Tok()]
            ob = [sb(K, st, [128, L], BF16, "obf") for _ in range(2)]
            obt = [Tok(), Tok()]
            oi = 0
            gq = 0
            it = 0
            ev = Evac(K)
            for g in range(4):
                wc, wct = wcs[0], wcst[0]
                for j in range(2):
                    S.dma("pool", wc[:, j, :, :], K.WcD[j, g].rearrange("(kc p) n -> p kc n", p=128), writes=[wct])
                for s_ in range(nseq):
                    u, ut = uT[0], uTt[0]
                    it += 1
                    ts = t0 + s_ * L
                    S.dma("sp", u[:], K.uT[g * FG:(g + 1) * FG, ts:ts + L].rearrange("(kc p) t -> p kc t", p=128), writes=[ut])
                    for j in range(2):
                        for lc in range(LC):
                            bank = gq % 8
                            gq += 1

                            def mm():
                                ins = None
                                for cc in range(GC):
                                    ins = nc.tensor.matmul(ps[:, bank, 0:FG], u[:, cc, lc * 128:(lc + 1) * 128], wc[:, j, cc, :],
                                                           start=(cc == 0), stop=(cc == GC - 1))
                                return ins

                            S.op("pe", mm, reads=[ut, wct], writes=[pstoks[bank]])
                            ev.copy(Pcs[:, j, lc, :], ps[:, bank, 0:FG], [pstoks[bank]], [Pt[j]])
                    gq = ((gq + nb - 1) // nb) * nb
                    for dch in range(GC):
                        slot = (gq // nb) % (8 // nb)
                        gq += nb
                        banks = list(range(slot * nb, slot * nb + nb))

                        def mm2():
                            ins = None
                            for tb in range(nb):
                                tw = min(512, L - tb * 512)
                                n = 0
                                for j in range(2):
                                    for lc in range(LC):
                                        ins = nc.tensor.matmul(ps[:, banks[tb], 0:tw], Pcs[:, j, lc, dch * 128:(dch + 1) * 128],
                                                               cl[:, j, lc, tb * 512:tb * 512 + tw], start=(n == 0), stop=(n == 2 * LC - 1))
                                        n += 1
                            return ins

                        toks = [pstoks[b] for b in banks]
                        S.op("pe", mm2, reads=[Pt[0], Pt[1], clt[0], clt[1]], writes=toks)
                        i = oi % 2
                        oi += 1
                        pv = ps[:, banks[0]:banks[0] + nb, :].rearrange("p b n -> p (b n)")[:, 0:L]
                        ev.copy(ob[i][:, 0:L], pv, toks, [obt[i]])
                        r0 = g * FG + dch * 128
                        S.dma("sp", K.yT[r0:r0 + 128, ts:ts + L], ob[i][:, 0:L], reads=[obt[i]])
        S.barrier()


def phase_poolmix(K):
    S, nc, C = K.S, K.nc, K.cfg
    D, FG = C.D, C.FG
    GC = FG // 128
    with ExitStack() as st:
        psc = sb(K, st, [128, D // 256], F32, "psc")
        psct = Tok()
        with ExitStack() as st2:
            pst = psum(K, st2, [128, 128], F32)
            load_cols(K, st2, psc, psct, K.d["pool_scale"], D // 256, pst, Tok())
            S.barrier()
        Tmax = max(tl["T"] for tl in K.tiles)
        ps = psum(K, st, [128, 8, 512], F32)
        pstoks = [Tok() for _ in range(8)]
        aT = sb(K, st, [128, GC, Tmax], BF16, "aTp")
        atoks = [Tok()]
        ws = WStream(K, st, GC, 512)
        ob = [sb(K, st, [128, Tmax], BF16, "obp") for _ in range(2)]
        obt = [Tok(), Tok()]
        oi = [0]
        for tl in K.tiles:
            T, t0 = tl["T"], tl["t0"]
            for g in range(4):
                r0 = D // 2 + g * FG
                load_aT(K, aT, atoks, K.uT[r0:r0 + FG, :], GC, t0, T)
                wsrcs = [(K.d["w_pool"][g, :, n0:n0 + wn], wn, n0) for n0, wn in col_tiles(0, FG, 512)]

                def epi(n0, m, pv, ptoks, T=T, t0=t0, g=g, r0=r0):
                    i = oi[0] % 2
                    oi[0] += 1
                    col = (g * FG + n0 + m * 128) // 128
                    S.op("act", lambda: nc.scalar.activation(out=ob[i][:, 0:T], in_=pv, func=AF.Copy, scale=psc[:, col:col + 1]),
                         reads=ptoks + [psct], writes=[obt[i]])
                    rr = r0 + n0 + m * 128
                    S.dma("sp", K.yT[rr:rr + 128, t0:t0 + T], ob[i][:, 0:T], reads=[obt[i]])

                gemm_fm(K, aT, atoks, GC, T, wsrcs, ws, ps, pstoks, epi)
    S.barrier()


def phase_qkv(K):
    S, nc, C = K.S, K.nc, K.cfg
    D, KC = C.D, C.KC
    wq = K.d["w_qkv"]
    with ExitStack() as st:
        Tmax = max(tl["T"] for tl in K.tiles)
        ps = psum(K, st, [128, 8, 512], F32)
        pstoks = [Tok() for _ in range(8)]
        aT = sb(K, st, [128, KC, Tmax], BF16, "aTq")
        atoks = [Tok() for _ in range(4)]
        ws = WStream(K, st, KC, 256)
        ob = [sb(K, st, [128, Tmax], BF16, "obq") for _ in range(2)]
        obt = [Tok(), Tok()]
        of = [sb(K, st, [128, 256], F32, "ofq") for _ in range(2)]
        oft = [Tok(), Tok()]
        oi = [0]
        ev = Evac(K)
        import os
        for tl in K.tiles:
            if os.environ.get('QKV_TILE', tl["name"]) != tl["name"]:
                continue
            T, t0, L = tl["T"], tl["t0"], tl["L"]
            load_aT(K, aT, atoks, K.hT, KC, t0, T)
            wsrcs = [(wq[:, n0:n0 + wn], wn, n0) for n0, wn in col_tiles(0, 2 * D, 256)]

            def epi(n0, m, pv, ptoks, T=T, t0=t0):
                f0 = n0 + m * 128
                i = oi[0] % 2
                oi[0] += 1
                if f0 < D:
                    ev.copy(ob[i][:, 0:T], pv, ptoks, [obt[i]], scale=float(128 ** -0.5))
                    S.dma("sp", K.qT[f0:f0 + 128, t0:t0 + T], ob[i][:, 0:T], reads=[obt[i]])
                else:
                    ev.copy(ob[i][:, 0:T], pv, ptoks, [obt[i]])
                    S.dma("sp", K.kT[f0 - D:f0 - D + 128, t0:t0 + T], ob[i][:, 0:T], reads=[obt[i]])

            import os
            if os.environ.get('QKV_PART', 'both') in ('both', 'fm'):
                gemm_fm(K, aT, atoks, KC, T, wsrcs, ws, ps, pstoks, epi)
            lo = D if (tl["name"] == "P" and os.environ.get("QKV_LO", "") != "2D") else 2 * D
            wsrcs = [(wq[:, n0:n0 + wn], wn, n0) for n0, wn in col_tiles(lo, 3 * D, 256)]

            def epi2(n0, tb, M, pv, ptok, T=T, t0=t0, tl=tl, L=L):
                i = oi[0] % 2
                oi[0] += 1
                tt = tb * 128
                isP = tl["name"] == "P"
                src, stok = pv, ptok
                if isP:
                    S.op("dve", lambda: nc.vector.tensor_copy(out=of[i][0:M, :], in_=pv), reads=[ptok], writes=[oft[i]])
                    src, stok = of[i][0:M, :], oft[i]
                if n0 >= 2 * D:
                    c0 = n0 - 2 * D
                    S.op("act", lambda: nc.scalar.copy(out=ob[i][0:M, 0:256], in_=src), reads=[stok], writes=[obt[i]])
                    S.dma("sp", K.vtm[t0 + tt:t0 + tt + M, c0:c0 + 256], ob[i][0:M, 0:256], reads=[obt[i]])
                if isP:
                    dst = K.d["nv"] if n0 >= 2 * D else K.d["nk"]
                    c0 = n0 - 2 * D if n0 >= 2 * D else n0 - D
                    b, tq = tt // L, tt % L
                    h0 = c0 // 128
                    S.dma("sp", dst[b, h0:h0 + 2, tq:tq + M, :].rearrange("h t d -> t h d"),
                          of[i][0:M, :].rearrange("t (h d) -> t h d", h=2), reads=[oft[i]])

            if os.environ.get('QKV_PART', 'both') in ('both', 'tm'):
                gemm_tm(K, aT, atoks, KC, T, wsrcs, ws, ps, pstoks, epi2)
    S.barrier()


def phase_attn(K):
    S, nc, C = K.S, K.nc, K.cfg
    LS, LP, NP, TOK, ROWS, PAST = C.LS, C.LP, C.NP, C.TOK, C.ROWS, C.PAST
    PC = PAST // 128
    NCH = 4 + PC
    kr = 8
    with ExitStack() as st:
        ps = psum(K, st, [128, 6, 512], F32)
        pstoks = [Tok() for _ in range(6)]
        pT = psum(K, st, [128, PC, 128], BF16)
        pTt = Tok()
        mask = sb(K, st, [128, 64], F32, "mask")
        maskt = Tok()
        S.dma("sp", mask[:], K.d["maskadd"], writes=[maskt])
        sets = []
        for i in range(2):
            d = {}
            for nm, shape, dt in (("q", [128, TOK], BF16), ("k", [128, TOK], BF16), ("v0", [128, TOK // 128, 128], BF16),
                                  ("v1", [128, LS // 128, 128], BF16), ("kc", [128, PC, 128], BF16), ("vc", [128, PC, 128], BF16),
                                  ("kcT", [128, PAST], BF16), ("rb0", [128, 7, 64], F32), ("rb1", [128, 7, 64], F32),
                                  ("rbb", [128, 2, 7, 64], BF16), ("o", [128, TOK], BF16)):
                d[nm] = (sb(K, st, shape, dt, "at" + nm), Tok())
            sets.append(d)
        E = [(sb(K, st, [128, 512], BF16, "E"), Tok()) for _ in range(2)]
        rd = [(sb(K, st, [128, 256], F32, "rd"), Tok()) for _ in range(2)]
        it = 0
        for h in range(C.NH):
            d = sets[h % 2]
            hs = slice(h * 128, (h + 1) * 128)
            S.dma("sp", d["q"][0][:], K.qT[hs, :], writes=[d["q"][1]])
            S.dma("sp", d["k"][0][:], K.kT[hs, :], writes=[d["k"][1]])
            S.dma("sp", d["v0"][0][:], K.vtm[:, hs].rearrange("(c p) d -> p c d", p=128), writes=[d["v0"][1]])
            S.dma("sp", d["v1"][0][:, 0:LS // 128 - 1, :], K.vtm[64:LS - 64, hs].rearrange("(c p) d -> p c d", p=128), writes=[d["v1"][1]])
            S.dma("pool", d["kc"][0][:], K.d["ck"][h].rearrange("(c p) d -> p c d", p=128), writes=[d["kc"][1]])
            S.dma("pool", d["vc"][0][:], K.d["cv"][h].rearrange("(c p) d -> p c d", p=128), writes=[d["vc"][1]])
            for j in range(2):
                S.dma("sp", d["rb0"][0][j * 64:(j + 1) * 64, :, :], K.d["rpbT"][h, j:14 + j:2, :, :].rearrange("r k q -> k r q"), writes=[d["rb0"][1]])
                S.dma("sp", d["rb1"][0][j * 64:(j + 1) * 64, :, :], K.d["rpbT"][h, 1 + j:15:2, :, :].rearrange("r k q -> k r q"), writes=[d["rb1"][1]])
            for par, nm in enumerate(("rb0", "rb1")):
                for r2 in range(7):
                    S.op("dve", lambda: nc.vector.tensor_tensor(out=d["rbb"][0][:, par, r2, :], in0=d[nm][0][:, r2, :], in1=mask[:], op=ALU.add),
                         reads=[d[nm][1], maskt], writes=[d["rbb"][1]])

            def tr():
                ins = None
                for c in range(PC):
                    ins = nc.tensor.transpose(out=pT[:, c, :], in_=d["kc"][0][:, c, :], identity=K.identB[:])
                return ins

            S.op("pe", tr, reads=[d["kc"][1], K.const_tok], writes=[pTt])
            S.op("act", lambda: nc.scalar.copy(out=d["kcT"][0][:], in_=pT[:].rearrange("p c t -> p (c t)")), reads=[pTt], writes=[d["kcT"][1]])
            q, k, o = d["q"][0], d["k"][0], d["o"][0]
            for r in range(ROWS):
                rs = min(max(r - kr // 2, 0), ROWS - kr)
                delta = rs - r
                a = it % 2
                it += 1
                sbank, obank, dbank = a, 2 + a, 4 + a
                Et, Ett = E[a]
                qs = q[:, r * 64:(r + 1) * 64]

                def sc():
                    ins = None
                    for c in range(4):
                        tk0 = rs * 64 + c * 128
                        dr0 = delta + 7 + 2 * c
                        rb = d["rbb"][0][:, dr0 % 2, dr0 // 2, :]
                        nc.tensor.matmul(ps[:, sbank, c * 64:(c + 1) * 64], k[:, tk0:tk0 + 128], qs, start=True, stop=False)
                        ins = nc.tensor.matmul(ps[:, sbank, c * 64:(c + 1) * 64], K.identB[:], rb, start=False, stop=True)
                    for c in range(PC):
                        ins = nc.tensor.matmul(ps[:, sbank, (4 + c) * 64:(5 + c) * 64], d["kcT"][0][:, c * 128:(c + 1) * 128], qs, start=True, stop=True)
                    return ins

                S.op("pe", sc, reads=[d["q"][1], d["k"][1], d["rbb"][1], d["kcT"][1], K.const_tok], writes=[pstoks[sbank]])
                S.op("act", lambda: nc.scalar.activation(out=Et[:, 0:NCH * 64], in_=ps[:, sbank, 0:NCH * 64], func=AF.Exp),
                     reads=[pstoks[sbank]], writes=[Ett])

                def pv_():
                    ins = None
                    for c in range(NCH):
                        if c < 4:
                            vch = d["v0"][0][:, rs // 2 + c, :] if rs % 2 == 0 else d["v1"][0][:, (rs - 1) // 2 + c, :]
                        else:
                            vch = d["vc"][0][:, c - 4, :]
                        ins = nc.tensor.matmul(ps[:, obank, 0:64], vch, Et[:, c * 64:(c + 1) * 64], start=(c == 0), stop=(c == NCH - 1))
                    for c in range(NCH):
                        ins = nc.tensor.matmul(ps[:, dbank, 0:64], K.onesB[:], Et[:, c * 64:(c + 1) * 64], start=(c == 0), stop=(c == NCH - 1))
                    return ins

                S.op("pe", pv_, reads=[Ett, d["v0"][1], d["v1"][1], d["vc"][1], K.const_tok], writes=[pstoks[obank], pstoks[dbank]])
                rdt, rdtt = rd[a]
                S.op("dve", lambda: nc.vector.reciprocal(out=rdt[:, 0:64], in_=ps[:, dbank, 0:64]), reads=[pstoks[dbank]], writes=[rdtt])
                S.op("dve", lambda: nc.vector.tensor_tensor(out=o[:, r * 64:(r + 1) * 64], in0=ps[:, obank, 0:64], in1=rdt[:, 0:64], op=ALU.mult),
                     reads=[pstoks[obank], rdtt], writes=[d["o"][1]])
            LPC = LP // 128
            for b in range(NP):
                tq0 = LS + b * LP
                a = it % 2
                it += 1
                sbank, obank, dbank = a, 2 + a, 4 + a
                Et, Ett = E[a]
                qs = q[:, tq0:tq0 + LP]

                def sc2():
                    ins = None
                    for c in range(LPC):
                        ins = nc.tensor.matmul(ps[:, sbank, c * LP:(c + 1) * LP], k[:, tq0 + c * 128:tq0 + (c + 1) * 128], qs, start=True, stop=True)
                    return ins

                S.op("pe", sc2, reads=[d["q"][1], d["k"][1]], writes=[pstoks[sbank]])
                S.op("act", lambda: nc.scalar.activation(out=Et[:, 0:LPC * LP], in_=ps[:, sbank, 0:LPC * LP], func=AF.Exp),
                     reads=[pstoks[sbank]], writes=[Ett])

                def pv2():
                    ins = None
                    for c in range(LPC):
                        ins = nc.tensor.matmul(ps[:, obank, 0:LP], d["v0"][0][:, tq0 // 128 + c, :], Et[:, c * LP:(c + 1) * LP], start=(c == 0), stop=(c == LPC - 1))
                    for c in range(LPC):
                        ins = nc.tensor.matmul(ps[:, dbank, 0:LP], K.onesB[:], Et[:, c * LP:(c + 1) * LP], start=(c == 0), stop=(c == LPC - 1))
                    return ins

                S.op("pe", pv2, reads=[Ett, d["v0"][1], K.const_tok], writes=[pstoks[obank], pstoks[dbank]])
                rdt, rdtt = rd[a]
                S.op("dve", lambda: nc.vector.reciprocal(out=rdt[:, 0:LP], in_=ps[:, dbank, 0:LP]), reads=[pstoks[dbank]], writes=[rdtt])
                S.op("dve", lambda: nc.vector.tensor_tensor(out=o[:, tq0:tq0 + LP], in0=ps[:, obank, 0:LP], in1=rdt[:, 0:LP], op=ALU.mult),
                     reads=[pstoks[obank], rdtt], writes=[d["o"][1]])
            S.dma("sp", K.yT[hs, :], o[:], reads=[d["o"][1]])
    S.barrier()


def input_shapes(C):
    D, DFF = C.D, C.DFF
    return {
        "x_s": [C.LS, D], "x_p": [C.NP * C.LP, D], "ck": [C.NH, C.PAST, 128], "cv": [C.NH, C.PAST, 128],
        "cond": [2, D], "w_ada": [C.depth, D, 6 * D], "b_ada": [C.depth, 6 * D],
        "ln1_g": [C.depth, D], "ln1_b": [C.depth, D], "ln2_g": [C.depth, D], "ln2_b": [C.depth, D],
        "w_in": [D, D], "w_four": [4, C.FG, C.FG], "w_pool": [4, C.FG, C.FG], "pool_scale": [D // 2],
        "w_out_a": [D, D], "w_qkv": [D, 3 * D], "rpbT": [C.NH, 15, 64, 64], "w_out_c": [D, D],
        "w_up": [C.depth, D, 2 * DFF], "conv_w": [C.depth, 3, 2 * DFF], "conv_b": [C.depth, 2 * DFF], "w_down": [C.depth, DFF, D],
        "ident": [128, 128], "CC": [C.FG, C.FG], "SC": [C.FG, C.FG], "CLS": [C.LS, C.LS], "SLnS": [C.LS, C.LS],
        "CLP": [C.LP, C.LP], "SLnP": [C.LP, C.LP], "maskadd": [128, 64], "pool_edge": [64],
    }


def build(C, upto=99, skip=()):
    nc = bass.Bass("TRN2", target_bir_lowering=False)
    K = Ctx()
    K.nc, K.cfg = nc, C
    D, TOK = C.D, C.TOK
    K.d = {n: nc.dram_tensor(n, s, F32, kind="ExternalInput").ap() for n, s in input_shapes(C).items()}
    for n, s in (("y_s", [C.LS, D]), ("y_p", [C.NP * C.LP, D]), ("nk", [C.NP, C.NH, C.LP, 128]), ("nv", [C.NP, C.NH, C.LP, 128])):
        K.d[n] = nc.dram_tensor(n, s, F32, kind="ExternalOutput").ap()

    def scratch(n, s, dt):
        return nc.dram_tensor(n, s, dt, kind="Internal").ap()

    K.modD = scratch("modD", [C.depth, 2, 6 * D], F32)
    K.xs = scratch("xs", [TOK, D], F32)
    K.ypre = scratch("ypre", [TOK, D], F32)
    K.hT = scratch("hT", [D, TOK], BF16)
    K.uT = scratch("uT", [D, TOK], BF16)
    K.yT = scratch("yT", [D, TOK], BF16)
    K.qT = scratch("qT", [D, TOK], BF16)
    K.kT = scratch("kT", [D, TOK], BF16)
    K.vtm = scratch("vtm", [TOK, D], BF16)
    K.actT = scratch("actT", [C.DFF, TOK], BF16)
    K.WcD = scratch("WcD", [2, 4, C.FG, C.FG], F32)
    K.tiles = [dict(name="S", t0=0, T=C.LS, L=C.LS, nseq=1, cond=0),
               dict(name="P", t0=C.LS, T=C.NP * C.LP, L=C.LP, nseq=C.NP, cond=1)]

    def xin(t0, n):
        if t0 < C.LS:
            return K.d["x_s"][t0:t0 + n, :]
        return K.d["x_p"][t0 - C.LS:t0 - C.LS + n, :]

    def xout(t0, n):
        if t0 < C.LS:
            return K.d["y_s"][t0:t0 + n, :]
        return K.d["y_p"][t0 - C.LS:t0 - C.LS + n, :]

    def xscr(t0, n):
        return K.xs[t0:t0 + n, :]

    with ExitStack() as st:
        K.S = Sch(nc, st)
        g = lambda n, i: K.d[n][i, :]
        phases = [
            lambda: phase_consts(K, st),
            lambda: phase_mod(K),
            lambda: phase_fourier_weights(K),
            lambda: ln_pass(K, K.tiles, None, (0, 1, 0), xin, None),
            lambda: phase_win(K),
            lambda: phase_fourier(K),
            lambda: phase_poolmix(K),
            lambda: gemm_out_tm(K, K.yT, K.d["w_out_a"], 0, 2),
            lambda: ln_pass(K, K.tiles, (0, 2, g("ln1_g", 0), g("ln1_b", 0)), (0, 4, 3), xin, xscr),
            lambda: phase_ffn(K, 0),
            lambda: ln_pass(K, K.tiles, (0, 5, g("ln2_g", 0), g("ln2_b", 0)), (1, 1, 0), xscr, xscr),
            lambda: phase_qkv(K),
            lambda: phase_attn(K),
            lambda: gemm_out_tm(K, K.yT, K.d["w_out_c"], 1, 2),
            lambda: ln_pass(K, K.tiles, (1, 2, g("ln1_g", 1), g("ln1_b", 1)), (1, 4, 3), xscr, xscr),
            lambda: phase_ffn(K, 1),
            lambda: ln_pass(K, K.tiles, (1, 5, g("ln2_g", 1), g("ln2_b", 1)), None, xscr, xout),
        ]
        for i, ph in enumerate(phases):
            if i < upto and i not in skip:
                ph()
        K.S.barrier()
    return nc


def host_consts(C):
    def dft(n):
        k = np.arange(n)
        ang = 2.0 * np.pi * ((k[:, None] * k[None, :]) % n) / n
        s = 1.0 / np.sqrt(n)
        return (np.cos(ang) * s).astype(np.float32), (np.sin(ang) * s).astype(np.float32)

    out = {"ident": np.eye(128, dtype=np.float32)}
    out["CC"], out["SC"] = dft(C.FG)
    c, s_ = dft(C.LS)
    out["CLS"], out["SLnS"] = c, -s_
    c, s_ = dft(C.LP)
    out["CLP"], out["SLnP"] = c, -s_
    kc = np.arange(64)[:, None]
    qc = np.arange(64)[None, :]
    cs = np.clip(qc - 8, 0, 48)
    valid = (kc >= cs) & (kc < cs + 16)
    m = np.where(valid, 0.0, -30000.0).astype(np.float32)
    out["maskadd"] = np.concatenate([m, m], axis=0)
    edge = np.ones((4, 2, 8), np.float32)
    for wi, w in enumerate((2, 4, 8, 16)):
        for t in range(w // 2):
            edge[wi, 0, t] = 1.0 / (t + w // 2)
        for j in range(1, w // 2):
            edge[wi, 1, 8 - j] = 1.0 / (j + w // 2)
    out["pool_edge"] = edge.reshape(-1)
    return out


def make_in_maps(C, inputs, n_cores):
    f = lambda a: np.ascontiguousarray(np.asarray(a, dtype=np.float32))
    shared = {}
    for n in ("w_ada", "b_ada", "ln1_g", "ln1_b", "ln2_g", "ln2_b", "w_up", "conv_w", "conv_b", "w_down"):
        shared[n] = f(inputs[n])
    for n in ("w_in", "w_four", "w_pool", "pool_scale", "w_out_a", "w_qkv", "w_out_c"):
        shared[n] = f(inputs[n])[0]
    rpb = f(inputs["rpb"])[0]
    kc = np.arange(64)[:, None]
    qc = np.arange(64)[None, :]
    dc = np.clip(kc - qc + 15, 0, 30)
    shared["rpbT"] = np.ascontiguousarray(rpb[:, :, dc])
    shared.update(host_consts(C))
    xp, xs_ = f(inputs["x_prompt"]), f(inputs["x_sample"])
    ck, cv, c, cctx = f(inputs["cache_k"]), f(inputs["cache_v"]), f(inputs["c"]), f(inputs["c_ctx"])
    maps = []
    for i in range(n_cores):
        m = dict(shared)
        m["x_s"] = xs_[i]
        m["x_p"] = np.ascontiguousarray(xp[i * C.NP:(i + 1) * C.NP].reshape(C.NP * C.LP, C.D))
        m["ck"] = ck[i, 0]
        m["cv"] = cv[i, 0]
        m["cond"] = np.ascontiguousarray(np.stack([c[i], cctx], axis=0))
        maps.append(m)
    return maps


_CACHE = {}


def kernel(**inputs):
    n_cores = 8
    C = Cfg()
    if "nc" not in _CACHE:
        _CACHE["nc"] = build(C)
    nc = _CACHE["nc"]
    maps = make_in_maps(C, inputs, n_cores)
    res = run_bass_kernel_spmd(nc, maps, core_ids=list(range(n_cores)))
    R = res.results
    y_s = np.stack([R[i]["y_s"] for i in range(n_cores)], axis=0)
    y_p = np.concatenate([R[i]["y_p"].reshape(C.NP, C.LP, C.D) for i in range(n_cores)], axis=0)
    nk = np.concatenate([R[i]["nk"] for i in range(n_cores)], axis=0)[:, None]
    nv = np.concatenate([R[i]["nv"] for i in range(n_cores)], axis=0)[:, None]
    return (y_p.astype(np.float32), y_s.astype(np.float32), nk.astype(np.float32), nv.astype(np.float32))
```
